# Optimizing a Trainium2 kernel written in Bass

```python
import math
import jax, jax.numpy as jnp
from jax import lax
import numpy as np

D_MODEL = 1024
BATCH = 4
SEQ = 8192
DEPTH = 1

EXPAND = 2
D_MIX = EXPAND * D_MODEL
RET_WIDTH = D_MIX // 2
MLA_WIDTH = D_MIX - RET_WIDTH
RET_V_DIM = 128
RET_QK_DIM = 64
RET_HEADS = RET_WIDTH // RET_V_DIM
MLA_V = 128
MLA_NOPE = 64
MLA_ROPE = 32
MLA_HEADS = MLA_WIDTH // MLA_V
Q_LORA = 384
KV_LORA = 256
CHUNK = 128
Q_BLOCK = 128
ROPE_BASE = 10000.0
EPS = 1e-6

SPLIT_SIZES = (
    RET_HEADS * RET_QK_DIM,
    RET_HEADS * RET_QK_DIM,
    RET_WIDTH,
    RET_WIDTH,
    Q_LORA,
    KV_LORA,
    MLA_ROPE,
    MLA_WIDTH,
)
D_IN = sum(SPLIT_SIZES)

kernel_name = "hybrid_retention_mla_block"


def rms_norm(x, w):
    xf = x.astype(jnp.float32)
    y = xf * lax.rsqrt(jnp.mean(xf * xf, axis=-1, keepdims=True) + EPS)
    return (y * w.astype(jnp.float32)).astype(x.dtype)


def rope(x, positions):
    d = x.shape[-1]
    inv_freq = ROPE_BASE ** (-jnp.arange(0, d, 2, dtype=jnp.float32) / d)
    ang = positions.astype(jnp.float32)[:, :, None] * inv_freq
    if x.ndim == 4:
        ang = ang[:, :, None, :]
    cos, sin = jnp.cos(ang), jnp.sin(ang)
    xf = x.astype(jnp.float32)
    x1, x2 = xf[..., : d // 2], xf[..., d // 2:]
    return jnp.concatenate([x1 * cos - x2 * sin, x2 * cos + x1 * sin], axis=-1).astype(x.dtype)


def split_cols(t, sizes):
    out, off = [], 0
    for s in sizes:
        out.append(t[..., off:off + s])
        off += s
    return out


def retention_dir(q, k, v, log_gamma, include_diag):
    B, H, S, dk = q.shape
    dv = v.shape[-1]
    n = S // CHUNK
    qc = q.reshape(B, H, n, CHUNK, dk)
    kc = k.reshape(B, H, n, CHUNK, dk)
    vc = v.reshape(B, H, n, CHUNK, dv)
    idx = jnp.arange(CHUNK, dtype=jnp.float32)
    diff = idx[:, None] - idx[None, :]
    mask = diff >= 0 if include_diag else diff > 0
    lg = log_gamma[:, None, None]
    decay_intra = jnp.where(mask[None], jnp.exp(lg * jnp.maximum(diff, 0.0)[None]), 0.0)
    scores = jnp.einsum('bhnik,bhnjk->bhnij', qc, kc) * decay_intra[None, :, None]
    intra = jnp.einsum('bhnij,bhnjv->bhniv', scores, vc)
    k_decay = jnp.exp(log_gamma[:, None] * (CHUNK - 1 - idx)[None])
    kv_chunk = jnp.einsum('bhnjk,bhnjv->bhnkv', kc * k_decay[None, :, None, :, None], vc)
    chunk_decay = jnp.exp(log_gamma * CHUNK)[None, :, None, None]

    def step(state, kv):
        return state * chunk_decay + kv, state

    init = jnp.zeros((B, H, dk, dv), jnp.float32)
    _, states = lax.scan(step, init, jnp.moveaxis(kv_chunk, 2, 0))
    states = jnp.moveaxis(states, 0, 2)
    q_decay = jnp.exp(log_gamma[:, None] * (idx + 1.0)[None])
    cross = jnp.einsum('bhnik,bhnkv->bhniv', qc * q_decay[None, :, None, :, None], states)
    return (intra + cross).reshape(B, H, S, dv)


def setup_inputs(seed: int = 0) -> dict:
    key = jax.random.key(seed)
    ks = jax.random.split(key, 24)
    f32 = jnp.float32

    def nrm(k, shape, scale):
        return jax.random.normal(k, shape, f32) * scale

    def gain(k, shape):
        return 1.0 + 0.02 * jax.random.normal(k, shape, f32)

    x = jax.random.normal(ks[0], (BATCH, SEQ, D_MODEL), f32)
    c = jax.random.normal(ks[1], (BATCH, D_MODEL), f32)
    offsets = jax.random.randint(ks[2], (BATCH, 1), 0, 4096, dtype=jnp.int32)
    positions = jnp.arange(SEQ, dtype=jnp.int32)[None, :] + offsets
    heads = np.arange(RET_HEADS, dtype=np.float32)
    base_logit = jnp.asarray(np.log(2.0 ** (5.0 + heads) - 1.0), f32)
    return {
        "x": x,
        "c": c,
        "positions": positions,
        "norm_w": gain(ks[3], (D_MODEL,)),
        "w_ada": nrm(ks[4], (D_MODEL, 3 * D_MODEL), 0.2 * D_MODEL ** -0.5),
        "b_ada": nrm(ks[5], (3 * D_MODEL,), 0.02),
        "w_in": nrm(ks[6], (D_MODEL, D_IN), D_MODEL ** -0.5),
        "ret_decay_logit_fwd": base_logit + 0.1 * jax.random.normal(ks[7], (RET_HEADS,), f32),
        "ret_decay_logit_bwd": base_logit + 0.1 * jax.random.normal(ks[8], (RET_HEADS,), f32),
        "ret_gn_w": gain(ks[9], (RET_HEADS, RET_V_DIM)),
        "q_norm_w": gain(ks[10], (Q_LORA,)),
        "w_uq": nrm(ks[11], (Q_LORA, MLA_HEADS * (MLA_NOPE + MLA_ROPE)), Q_LORA ** -0.5),
        "kv_norm_w": gain(ks[12], (KV_LORA,)),
        "w_ukv": nrm(ks[13], (KV_LORA, MLA_HEADS * (MLA_NOPE + MLA_V)), KV_LORA ** -0.5),
        "qn_nope_w": gain(ks[14], (MLA_NOPE,)),
        "qn_rope_w": gain(ks[15], (MLA_ROPE,)),
        "kn_nope_w": gain(ks[16], (MLA_NOPE,)),
        "kn_rope_w": gain(ks[17], (MLA_ROPE,)),
        "mla_out_norm_w": gain(ks[18], (MLA_WIDTH,)),
        "w_out": nrm(ks[19], (D_MIX, D_MODEL), D_MIX ** -0.5),
    }


def reference(x, c, positions, norm_w, w_ada, b_ada, w_in, ret_decay_logit_fwd,
              ret_decay_logit_bwd, ret_gn_w, q_norm_w, w_uq, kv_norm_w, w_ukv,
              qn_nope_w, qn_rope_w, kn_nope_w, kn_rope_w, mla_out_norm_w, w_out):
    f32 = jnp.float32
    B, S, _ = x.shape
    for _layer in range(DEPTH):
        mod = jax.nn.silu(c) @ w_ada + b_ada
        shift, scale, gate = jnp.split(mod, 3, axis=-1)
        h = rms_norm(x, norm_w) * (1.0 + scale[:, None, :]) + shift[:, None, :]

        proj = h @ w_in
        r_q, r_k, r_v, r_g, m_cq, m_ckv, m_kr, m_g = split_cols(proj, SPLIT_SIZES)

        rq = rope(r_q.reshape(B, S, RET_HEADS, RET_QK_DIM), positions)
        rk = rope(r_k.reshape(B, S, RET_HEADS, RET_QK_DIM), positions) * (RET_QK_DIM ** -0.5)
        rv = r_v.reshape(B, S, RET_HEADS, RET_V_DIM)
        rq = rq.transpose(0, 2, 1, 3).astype(f32)
        rk = rk.transpose(0, 2, 1, 3).astype(f32)
        rv = rv.transpose(0, 2, 1, 3).astype(f32)
        lg_f = jax.nn.log_sigmoid(ret_decay_logit_fwd.astype(f32))
        lg_b = jax.nn.log_sigmoid(ret_decay_logit_bwd.astype(f32))
        o_fwd = retention_dir(rq, rk, rv, lg_f, True)
        o_bwd = jnp.flip(retention_dir(jnp.flip(rq, 2), jnp.flip(rk, 2), jnp.flip(rv, 2),
                                       lg_b, False), 2)
        o = o_fwd + o_bwd
        mu = jnp.mean(o, axis=-1, keepdims=True)
        var = jnp.mean(jnp.square(o - mu), axis=-1, keepdims=True)
        o = (o - mu) * lax.rsqrt(var + EPS) * ret_gn_w.astype(f32)[None, :, None, :]
        o = o.transpose(0, 2, 1, 3).reshape(B, S, RET_WIDTH).astype(x.dtype)
        ret_out = o * jax.nn.silu(r_g)

        cq = rms_norm(m_cq, q_norm_w)
        q = (cq @ w_uq).reshape(B, S, MLA_HEADS, MLA_NOPE + MLA_ROPE)
        q_nope, q_rope = q[..., :MLA_NOPE], q[..., MLA_NOPE:]
        ckv = rms_norm(m_ckv, kv_norm_w)
        kv = (ckv @ w_ukv).reshape(B, S, MLA_HEADS, MLA_NOPE + MLA_V)
        k_nope, v = kv[..., :MLA_NOPE], kv[..., MLA_NOPE:]
        q_nope = rms_norm(q_nope, qn_nope_w)
        q_rope = rope(rms_norm(q_rope, qn_rope_w), positions)
        k_nope = rms_norm(k_nope, kn_nope_w)
        k_rope = rope(rms_norm(m_kr, kn_rope_w), positions)
        sm_scale = (MLA_NOPE + MLA_ROPE) ** -0.5
        nb = S // Q_BLOCK
        qn_blk = jnp.moveaxis(q_nope.reshape(B, nb, Q_BLOCK, MLA_HEADS, MLA_NOPE), 1, 0)
        qr_blk = jnp.moveaxis(q_rope.reshape(B, nb, Q_BLOCK, MLA_HEADS, MLA_ROPE), 1, 0)

        def attend(blk):
            qn, qr = blk
            s = (jnp.einsum('bqhd,bkhd->bhqk', qn, k_nope)
                 + jnp.einsum('bqhd,bkd->bhqk', qr, k_rope))
            p = jax.nn.softmax(s.astype(f32) * sm_scale, axis=-1).astype(v.dtype)
            return jnp.einsum('bhqk,bkhd->bqhd', p, v)

        att = lax.map(attend, (qn_blk, qr_blk))
        att = jnp.moveaxis(att, 0, 1).reshape(B, S, MLA_WIDTH)
        mla_out = rms_norm(att, mla_out_norm_w) * jax.nn.silu(m_g)

        y = jnp.concatenate([ret_out, mla_out], axis=-1) @ w_out
        x = x + gate[:, None, :] * y
    return x
```

```python
import math
from contextlib import ExitStack
import numpy as np
import concourse.bass as bass
import concourse.mybir as mybir
from concourse.bass_utils import run_bass_kernel_spmd

F32 = mybir.dt.float32
BF16 = mybir.dt.bfloat16
I32 = mybir.dt.int32
AF = mybir.ActivationFunctionType
ALU = mybir.AluOpType
AX = mybir.AxisListType

SAME_ENGINE_SYNC = True

D = 1024
DIN = 4768
C_RQ, C_RK, C_RV, C_RG, C_CQ, C_CKV, C_KR, C_MG = 0, 512, 1024, 2048, 3072, 3456, 3712, 3744
EPS = 1e-6
NCONST = 676
SM_SCALE = 96.0 ** -0.5
TWO_PI = 2.0 * math.pi
CW1 = 6.28125
CW2 = TWO_PI - CW1
MAGIC = 12582912.0
PI_S = 3.1415925


class Buf:
    __slots__ = ("name", "w", "r")

    def __init__(self, name):
        self.name = name
        self.w = None
        self.r = []


class TL:
    __slots__ = ("h", "b", "name")

    def __init__(self, h, name):
        self.h = h
        self.b = Buf(name)
        self.name = name

    def __getitem__(self, k):
        return self.h[k]


class Op:
    __slots__ = ("eng", "fn", "deps", "needed", "sem", "val", "stream", "blk")

    def __init__(self, eng, fn, stream, blk):
        self.eng = eng
        self.fn = fn
        self.deps = []
        self.needed = False
        self.sem = None
        self.val = 0
        self.stream = stream
        self.blk = blk


class Sched:
    ENGS = ("pe", "act", "dve", "pool", "sp")

    def __init__(self, nc, es):
        self.nc = nc
        self.es = es
        self.cur = {e: [] for e in self.ENGS}
        self.blk = 0
        self.sems = {}
        self.cnt = {}
        self.stream_last = {}
        self.persist = set()
        self.grouped = set()

    def sem(self, key):
        if key not in self.sems:
            self.sems[key] = self.es.enter_context(self.nc.semaphore(key))
            self.cnt[key] = 0
        return self.sems[key]

    def op(self, eng, fn, reads=(), writes=(), stream=None):
        o = Op(eng, fn, stream, self.blk)
        deps = []
        for b in reads:
            b = b.b if isinstance(b, TL) else b
            if b.w is not None:
                deps.append(b.w)
        for b in writes:
            b = b.b if isinstance(b, TL) else b
            if b.w is not None:
                deps.append(b.w)
            deps.extend(b.r)
        o.deps = deps
        for b in reads:
            b = b.b if isinstance(b, TL) else b
            b.r.append(o)
        for b in writes:
            b = b.b if isinstance(b, TL) else b
            b.w = o
            b.r = []
        self.cur[eng].append(o)
        return o

    def _skip(self, o, d):
        if d is o:
            return True
        if d.stream is None:
            if d.blk != o.blk:
                return True
            if d.eng == o.eng and o.stream is None and (d.eng == "pe" or not SAME_ENGINE_SYNC):
                return True
        return False

    def emit_block(self, name=None, final_waits=()):
        nc = self.nc
        ops = self.cur
        for e in self.ENGS:
            for o in ops[e]:
                for d in o.deps:
                    if not self._skip(o, d):
                        d.needed = True
        for e in self.ENGS:
            for o in ops[e]:
                if o.stream is not None:
                    key = "d_" + o.stream
                    o.sem = self.sem(key)
                    self.cnt[key] += 16
                    o.val = self.cnt[key]
                    o.needed = True
                    self.stream_last[o.stream] = (o.sem, o.val)
                elif o.needed:
                    key = e
                    o.sem = self.sem(key)
                    self.cnt[key] += 1
                    o.val = self.cnt[key]

        for e in self.ENGS:
            for o in ops[e]:
                if o.stream is not None and o.stream in self.grouped:
                    o.val = self.cnt["d_" + o.stream]

        def run(ename, eng):
            waited = {}
            for o in ops[ename]:
                need = {}
                for d in o.deps:
                    if self._skip(o, d) or d.sem is None:
                        continue
                    k = id(d.sem)
                    if d.val > waited.get(k, 0) and d.val > need.get(k, (None, 0))[1]:
                        need[k] = (d.sem, d.val)
                for k, (s, v) in need.items():
                    eng.wait_ge(s, v)
                    waited[k] = v
                ins = o.fn(eng)
                if o.needed and ins is not None:
                    ins.then_inc(o.sem, 16 if o.stream is not None else 1)

        drains = [(s, v) for st, (s, v) in self.stream_last.items() if st not in self.persist]
        with nc.Block() as block:
            @block.tensor
            def _(e):
                run("pe", e)

            @block.scalar
            def _(e):
                run("act", e)

            @block.vector
            def _(e):
                run("dve", e)

            @block.gpsimd
            def _(e):
                run("pool", e)

            @block.sync
            def _(e):
                run("sp", e)
                for (s, v) in drains:
                    e.wait_ge(s, v)
        self.cur = {e: [] for e in self.ENGS}
        self.blk += 1


def MM(out, lhsT, rhs, start=True, stop=True):
    return lambda e: e.matmul(out, lhsT=lhsT, rhs=rhs, start=start, stop=stop)


def TR(out, in_, ident):
    return lambda e: e.transpose(out=out, in_=in_, identity=ident)


def ACTF(out, in_, func, scale=None, bias=None, accum=None):
    kw = {}
    if scale is not None:
        kw["scale"] = scale
    if bias is not None:
        kw["bias"] = bias
    if accum is not None:
        kw["accum_out"] = accum
    return lambda e: e.activation(out=out, in_=in_, func=func, **kw)


def TS(out, in0, s1, s2=None, op0=ALU.mult, op1=None):
    if op1 is None:
        return lambda e: e.tensor_scalar(out=out, in0=in0, scalar1=s1, scalar2=None, op0=op0)
    return lambda e: e.tensor_scalar(out=out, in0=in0, scalar1=s1, scalar2=s2, op0=op0, op1=op1)


def TT(out, in0, in1, op):
    return lambda e: e.tensor_tensor(out=out, in0=in0, in1=in1, op=op)


def STT(out, in0, scalar, in1, op0, op1):
    return lambda e: e.scalar_tensor_tensor(out=out, in0=in0, scalar=scalar, in1=in1, op0=op0, op1=op1)


def CP(out, in_):
    return lambda e: e.tensor_copy(out=out, in_=in_)


def ACP(out, in_):
    return lambda e: e.copy(out=out, in_=in_)


def RED(out, in_, op=ALU.add):
    return lambda e: e.tensor_reduce(out=out, in_=in_, axis=AX.X, op=op)


def DMA(out, in_):
    return lambda e: e.dma_start(out=out, in_=in_)


def MSET(ap, v):
    return lambda e: e.memset(ap, v)


def bc(ap, shape):
    return ap.to_broadcast(list(shape))


def build_program(S_LEN, debug=False, stop=None, c2cut=99, c2mode=''):
    NB = S_LEN // 128
    NO = NB // 2
    TO = NO * 128
    QT = 1024
    NQT = TO // QT
    assert stop in ('setup', 'R', 'C2') or (NB % 8 == 0 and TO % QT == 0)

    nc = bass.Bass("TRN2", target_bir_lowering=False)

    def din(name, shape, dt=F32):
        return nc.dram_tensor(name, shape, dt, kind="ExternalInput").ap()

    def dscr(name, shape, dt):
        if debug:
            return nc.dram_tensor(name, shape, dt, kind="ExternalOutput").ap()
        return nc.dram_tensor(name, shape, dt).ap()

    xs = din("xs", [S_LEN, D])
    posT = din("posT", [128, NB], I32)
    consts = din("consts", [128, NCONST])
    cT_in = din("cT", [128, 8])
    nwT_in = din("nwT", [128, 8])
    mlawT_in = din("mlawT", [128, 8])
    w_ada = din("w_ada", [D, 3 * D])
    b_ada = din("b_ada", [1, 3 * D])
    w_in = din("w_in", [D, DIN])
    lg_up = din("lg_up", [1, 8])
    lg_dn = din("lg_dn", [1, 8])
    gnw_in = din("gnw", [1, 1024])
    qnw_in = din("qnw", [1, 384])
    kvnw_in = din("kvnw", [1, 256])
    qw_in = din("qw", [1, 96])
    kw_in = din("kw", [1, 96])
    w_uq = din("w_uq", [384, 768])
    w_ukv = din("w_ukv", [256, 1536])
    w_out = din("w_out", [2048, D])
    out = nc.dram_tensor("out", [TO, D], F32, kind="ExternalOutput").ap()

    kt_scr = dscr("kt_scr", [8, 96, S_LEN], BF16)
    v_scr = dscr("v_scr", [8, 128, NB, 128], BF16)
    qt_scr = dscr("qt_scr", [8, 96, TO], BF16)
    ret_scr = dscr("ret_scr", [NO, 128, 1024], BF16)
    mg_scr = dscr("mg_scr", [NO, 128, 1024], BF16)
    sst_scr = dscr("sst_scr", [NO, 128, 512], BF16)
    dbg = {}
    if debug:
        dbg["hT0"] = nc.dram_tensor("dbg_hT0", [128, 1024], BF16, kind="ExternalOutput").ap()
        dbg["tab"] = nc.dram_tensor("dbg_tab", [128, NB * 64], F32, kind="ExternalOutput").ap()
        dbg["attT"] = nc.dram_tensor("dbg_attT", [128, 8 * TO], BF16, kind="ExternalOutput").ap()
        dbg["misc"] = nc.dram_tensor("dbg_misc", [128, 4096], F32, kind="ExternalOutput").ap()

    es0 = ExitStack()
    with es0:
        S = Sched(nc, es0)
        S.grouped.update(["cst", "wA", "wB"])

        uid = [0]

        def sb(es, name, shape, dt):
            uid[0] += 1
            return TL(es.enter_context(nc.sbuf_tensor("s%d_%s" % (uid[0], name), shape, dt)), name)

        def pp(es, name, shape, dt):
            uid[0] += 1
            return TL(es.enter_context(nc.psum_tensor("p%d_%s" % (uid[0], name), shape, dt)), name)

        kt_b = [Buf("kt%d" % g) for g in range(NB // 4)]
        v_b = [Buf("v%d" % g) for g in range(NB // 4)]
        qt_b = [Buf("qt%d" % g) for g in range(NO // 4)]
        ret_b = [Buf("ret%d" % n) for n in range(NO)]
        mg_b = [Buf("mg%d" % n) for n in range(NO)]
        sst_b = [Buf("sst%d" % n) for n in range(NO)]

        gate_b = sb(es0, "gate_b", [128, 1024], F32)
        identb = sb(es0, "identb", [128, 128], BF16)
        onesb = sb(es0, "onesb", [128, 128], BF16)
        mlawT = sb(es0, "mlawT", [128, 8], F32)
        neghalf = sb(es0, "neghalf", [128, 16], F32)

        with ExitStack() as es1:
            constT = sb(es1, "constT", [128, NCONST], F32)
            w_in_bf = sb(es1, "w_in_bf", [128, 8 * DIN], BF16)
            w_ukv_bf = sb(es1, "w_ukv_bf", [128, 2 * 1536], BF16)
            w_uq_bf = sb(es1, "w_uq_bf", [128, 3 * 768], BF16)
            cosT = sb(es1, "cosT", [128, NB * 32], F32)
            sinT = sb(es1, "sinT", [128, NB * 32], F32)
            DT = sb(es1, "DT", [128, 1024], F32)
            DqT = sb(es1, "DqT", [128, 1024], F32)
            kdec = sb(es1, "kdec", [128, 16], F32)
            dec128 = sb(es1, "dec128", [128, 8], F32)
            AT = sb(es1, "AT", [128, 8], F32)
            shT = sb(es1, "shT", [128, 8], F32)
            gnw_b = sb(es1, "gnw_b", [128, 1024], F32)
            qnw_b = sb(es1, "qnw_b", [128, 384], F32)
            kvnw_b = sb(es1, "kvnw_b", [128, 256], F32)
            qw_b = sb(es1, "qw_b", [128, 96], F32)
            kw_b = sb(es1, "kw_b", [128, 96], F32)

            ident_f = constT[:, 0:128]
            U_c = constT[:, 128:256]
            L_c = constT[:, 256:384]
            Iv1 = constT[:, 384:512]
            Jv = constT[:, 512:640]
            invf = constT[:, 640:672]
            w_in_v = w_in_bf[:, :].rearrange("p (k n) -> p k n", k=8)
            w_in_d = w_in.rearrange("(k p) n -> p k n", p=128)

            with ExitStack() as es:
                lgraw = sb(es, "lgraw", [128, 16], F32)
                e1 = sb(es, "e1", [128, 16], F32)
                lgb = sb(es, "lgb", [128, 16], F32)
                lgsel = sb(es, "lgsel", [128, 8], F32)
                cTt = sb(es, "cTt", [128, 8], F32)
                nwT = sb(es, "nwT", [128, 8], F32)
                sc = sb(es, "sc", [128, 8], F32)
                scb = sb(es, "scb", [128, 1024], F32)
                bada_b = sb(es, "bada_b", [128, 3072], F32)
                modrep = sb(es, "modrep", [128, 3072], F32)
                wada = [sb(es, "wada%d" % i, [128, 8 * 256], F32) for i in range(2)]
                tmpa = [sb(es, "tmpa%d" % i, [128, 128], F32) for i in range(2)]
                tmpb = [sb(es, "tmpb%d" % i, [128, 128], F32) for i in range(2)]
                dtmp = sb(es, "dtmp", [128, 1024], F32)
                scT = sb(es, "scT", [128, 8], F32)
                posi = sb(es, "posi", [128, NB], I32)
                posf = sb(es, "posf", [128, NB], F32)
                ang = sb(es, "ang", [128, NB * 32], F32)
                kk = sb(es, "kk", [128, NB * 32], F32)
                r1 = sb(es, "r1", [128, NB * 32], F32)
                psg = [pp(es, "psg%d" % i, [128, 512], F32) for i in range(2)]

                S.op("sp", DMA(constT[:, :], consts), writes=[constT], stream="cst")
                lgraw_a, lgraw_b = Buf("lgraw_a"), Buf("lgraw_b")
                S.op("sp", DMA(lgraw[:, 0:8], lg_up.partition_broadcast(128)), writes=[lgraw_a], stream="cst")
                S.op("sp", DMA(lgraw[:, 8:16], lg_dn.partition_broadcast(128)), writes=[lgraw_b], stream="cst")
                S.op("sp", DMA(cTt[:, :], cT_in), writes=[cTt], stream="cst")
                S.op("sp", DMA(nwT[:, :], nwT_in), writes=[nwT], stream="cst")
                S.op("sp", DMA(mlawT[:, :], mlawT_in), writes=[mlawT], stream="cst")
                S.op("sp", DMA(posi[:, :], posT), writes=[posi], stream="cst")
                S.op("sp", DMA(bada_b[:, :], b_ada.partition_broadcast(128)), writes=[bada_b], stream="cst")
                S.op("sp", DMA(gnw_b[:, :], gnw_in.partition_broadcast(128)), writes=[gnw_b], stream="cst")
                S.op("sp", DMA(qnw_b[:, :], qnw_in.partition_broadcast(128)), writes=[qnw_b], stream="cst")
                S.op("sp", DMA(kvnw_b[:, :], kvnw_in.partition_broadcast(128)), writes=[kvnw_b], stream="cst")
                S.op("sp", DMA(qw_b[:, :], qw_in.partition_broadcast(128)), writes=[qw_b], stream="cst")
                S.op("sp", DMA(kw_b[:, :], kw_in.partition_broadcast(128)), writes=[kw_b], stream="cst")
                win_bufs = {}
                wgrp = "wA"
                for nm, c0, c1 in (("rkv", C_RK, C_RG), ("ckvkr", C_CKV, C_MG)):
                    win_bufs[nm] = Buf("win_" + nm)
                    S.op("pool", DMA(w_in_v[:, :, c0:c1], w_in_d[:, :, c0:c1]), writes=[win_bufs[nm]], stream=wgrp)
                wukv_b = Buf("wukv")
                S.op("pool", DMA(w_ukv_bf[:, :].rearrange("p (k n) -> p k n", k=2),
                                 w_ukv.rearrange("(k p) n -> p k n", p=128)), writes=[wukv_b], stream="wA")
                wgrp = "wB"
                for nm, c0, c1 in (("rq", C_RQ, C_RK), ("rgcq", C_RG, C_CKV), ("mg", C_MG, DIN)):
                    win_bufs[nm] = Buf("win_" + nm)
                    S.op("pool", DMA(w_in_v[:, :, c0:c1], w_in_d[:, :, c0:c1]), writes=[win_bufs[nm]], stream=wgrp)
                wuq_b = Buf("wuq")
                S.op("pool", DMA(w_uq_bf[:, :].rearrange("p (k n) -> p k n", k=3),
                                 w_uq.rearrange("(k p) n -> p k n", p=128)), writes=[wuq_b], stream="wB")
                S.op("dve", CP(identb[:, :], ident_f), reads=[constT], writes=[identb])
                S.op("dve", MSET(onesb[:, :], 1.0), writes=[onesb])
                S.op("dve", MSET(neghalf[:, :], -0.5), writes=[neghalf])
                S.op("act", ACTF(e1[:, :], lgraw[:, :], AF.Exp, scale=-1.0), reads=[lgraw_a, lgraw_b], writes=[e1])
                S.op("dve", TS(e1[:, :], e1[:, :], 1.0, op0=ALU.add), reads=[e1], writes=[e1])
                S.op("act", ACTF(lgb[:, :], e1[:, :], AF.Ln), reads=[e1], writes=[lgb])
                S.op("dve", TS(lgb[:, :], lgb[:, :], -1.0, op0=ALU.mult), reads=[lgb], writes=[lgb])
                for d_ in range(2):
                    src = lgb[:, d_ * 8:(d_ + 1) * 8].rearrange("p (pr e) -> p pr e", e=2)
                    S.op("dve", CP(lgsel[0:64, d_ * 4:(d_ + 1) * 4], src[0:64, :, 0]), reads=[lgb], writes=[lgsel])
                    S.op("dve", CP(lgsel[64:128, d_ * 4:(d_ + 1) * 4], src[64:128, :, 1]), reads=[lgb], writes=[lgsel])
                S.op("act", ACTF(dec128[:, :], lgsel[:, :], AF.Exp, scale=128.0), reads=[lgsel], writes=[dec128])
                S.op("act", ACTF(kdec[:, 0:8], lgb[:, 0:8], AF.Exp, scale=constT[:, 672:673]), reads=[lgb, constT], writes=[kdec])
                S.op("act", ACTF(kdec[:, 8:16], lgb[:, 8:16], AF.Exp, scale=constT[:, 673:674]), reads=[lgb, constT], writes=[kdec])
                S.op("dve", TS(kdec[:, :], kdec[:, :], 0.125, op0=ALU.mult), reads=[kdec], writes=[kdec])
                for h in range(8):
                    ta, tb = tmpa[h % 2], tmpb[h % 2]
                    S.op("dve", TS(ta[:, :], U_c, lgb[:, h:h + 1], op0=ALU.mult), reads=[constT, lgb], writes=[ta])
                    S.op("dve", STT(tb[:, :], L_c, lgb[:, 8 + h:9 + h], ta[:, :], ALU.mult, ALU.add), reads=[constT, lgb, ta], writes=[tb])
                    S.op("act", ACTF(DT[:, h * 128:(h + 1) * 128], tb[:, :], AF.Exp), reads=[tb], writes=[DT])
                S.op("dve", TS(DT[:, :], DT[:, :], 0.125, op0=ALU.mult), reads=[DT], writes=[DT])
                for pr in range(4):
                    S.op("act", ACTF(DqT[:, pr * 128:(pr + 1) * 128], Iv1, AF.Exp, scale=lgsel[:, pr:pr + 1]),
                         reads=[constT, lgsel], writes=[DqT])
                    S.op("act", ACTF(DqT[:, (4 + pr) * 128:(5 + pr) * 128], Jv, AF.Exp, scale=lgsel[:, 4 + pr:5 + pr]),
                         reads=[constT, lgsel], writes=[DqT])
                S.op("dve", TS(qw_b[:, :], qw_b[:, :], SM_SCALE, op0=ALU.mult), reads=[qw_b], writes=[qw_b])
                S.op("act", ACTF(sc[:, :], cTt[:, :], AF.Silu), reads=[cTt], writes=[sc])
                S.op("dve", CP(scb[:, :].rearrange("p (k m) -> p k m", k=8), bc(sc[:, :].unsqueeze(2), [128, 8, 128])),
                     reads=[sc], writes=[scb])
                w_ada_d = w_ada.rearrange("(k p) n -> p k n", p=128)
                for nt in range(12):
                    wa = wada[nt % 2]
                    S.op("sp", DMA(wa[:, :].rearrange("p (k n) -> p k n", k=8), w_ada_d[:, :, nt * 256:(nt + 1) * 256]),
                         writes=[wa], stream="wada%d" % (nt % 2))
                    pg = psg[nt % 2]
                    for k in range(8):
                        S.op("pe", MM(pg[:, 0:256], scb[:, k * 128:(k + 1) * 128], wa[:, k * 256:(k + 1) * 256], start=(k == 0), stop=(k == 7)),
                             reads=[scb, wa], writes=[pg])
                    S.op("dve", TT(modrep[:, nt * 256:(nt + 1) * 256], pg[:, 0:256], bada_b[:, nt * 256:(nt + 1) * 256], ALU.add),
                         reads=[pg, bada_b], writes=[modrep])
                idb3 = bc(ident_f.unsqueeze(1), [128, 8, 128])
                S.op("dve", TT(dtmp[:, :].rearrange("p (k i) -> p k i", k=8), modrep[:, 0:1024].rearrange("p (k i) -> p k i", k=8), idb3, ALU.mult),
                     reads=[modrep, constT], writes=[dtmp])
                S.op("dve", RED(shT[:, :], dtmp[:, :].rearrange("p (k i) -> p k i", k=8)), reads=[dtmp], writes=[shT])
                S.op("dve", TT(dtmp[:, :].rearrange("p (k i) -> p k i", k=8), modrep[:, 1024:2048].rearrange("p (k i) -> p k i", k=8), idb3, ALU.mult),
                     reads=[modrep, constT], writes=[dtmp])
                S.op("dve", RED(scT[:, :], dtmp[:, :].rearrange("p (k i) -> p k i", k=8)), reads=[dtmp], writes=[scT])
                S.op("dve", STT(AT[:, :], scT[:, :], 1.0, nwT[:, :], ALU.add, ALU.mult), reads=[scT, nwT], writes=[AT])
                S.op("act", ACP(gate_b[:, :], modrep[:, 2048:3072]), reads=[modrep], writes=[gate_b])
                S.op("dve", CP(posf[:, :], posi[:, :]), reads=[posi], writes=[posf])
                ang3 = ang[:, :].rearrange("p (n f) -> p n f", f=32)
                S.op("dve", TT(ang3, bc(posf[:, :].unsqueeze(2), [128, NB, 32]), bc(invf.unsqueeze(1), [128, NB, 32]), ALU.mult),
                     reads=[posf, constT], writes=[ang])
                S.op("dve", TS(kk[:, :], ang[:, :], 1.0 / TWO_PI, MAGIC, op0=ALU.mult, op1=ALU.add), reads=[ang], writes=[kk])
                S.op("dve", TS(kk[:, :], kk[:, :], -MAGIC, op0=ALU.add), reads=[kk], writes=[kk])
                S.op("dve", STT(r1[:, :], kk[:, :], -CW1, ang[:, :], ALU.mult, ALU.add), reads=[kk, ang], writes=[r1])
                S.op("dve", STT(r1[:, :], kk[:, :], -CW2, r1[:, :], ALU.mult, ALU.add), reads=[kk, r1], writes=[r1])
                S.op("dve", TS(r1[:, :], r1[:, :], -PI_S, PI_S, op0=ALU.max, op1=ALU.min), reads=[r1], writes=[r1])
                S.op("act", ACTF(sinT[:, :], r1[:, :], AF.Sin), reads=[r1], writes=[sinT])
                S.op("dve", TS(kk[:, :], r1[:, :], -1.0, op0=ALU.mult), reads=[r1], writes=[kk])
                S.op("dve", TT(kk[:, :], kk[:, :], r1[:, :], ALU.max), reads=[r1, kk], writes=[kk])
                S.op("dve", TS(kk[:, :], kk[:, :], -math.pi / 2, op0=ALU.add), reads=[kk], writes=[kk])
                S.op("act", ACTF(cosT[:, :], kk[:, :], AF.Sin, scale=-1.0), reads=[kk], writes=[cosT])
                if debug:
                    S.op("sp", DMA(dbg["tab"][:, 0:NB * 32], cosT[:, :]), reads=[cosT], stream="dbg")
                    S.op("sp", DMA(dbg["tab"][:, NB * 32:NB * 64], sinT[:, :]), reads=[sinT], stream="dbg")
                    S.op("sp", DMA(dbg["misc"][:, 0:1024], DT[:, :]), reads=[DT], stream="dbg")
                    S.op("sp", DMA(dbg["misc"][:, 1024:2048], DqT[:, :]), reads=[DqT], stream="dbg")
                    S.op("sp", DMA(dbg["misc"][:, 2048:3072], gate_b[:, :]), reads=[gate_b], stream="dbg")
                    S.op("sp", DMA(dbg["misc"][:, 3072:3080], AT[:, :]), reads=[AT], stream="dbg")
                    S.op("sp", DMA(dbg["misc"][:, 3080:3088], shT[:, :]), reads=[shT], stream="dbg")
                    S.op("sp", DMA(dbg["misc"][:, 3088:3104], kdec[:, :]), reads=[kdec], stream="dbg")
                    S.op("sp", DMA(dbg["misc"][:, 3104:3112], dec128[:, :]), reads=[dec128], stream="dbg")
                S.emit_block()
            if stop == "setup":
                return nc

            cos3 = cosT[:, :].rearrange("p (n f) -> p n f", f=32)
            sin3 = sinT[:, :].rearrange("p (n f) -> p n f", f=32)

            def make_common(es):
                W = {}
                W["xt"] = [sb(es, "xt%d" % i, [128, 1024], F32) for i in range(2)]
                W["xn"] = [sb(es, "xn%d" % i, [128, 1024], BF16) for i in range(2)]
                W["hT"] = [sb(es, "hT%d" % i, [128, 1024], BF16) for i in range(2)]
                W["junk"] = sb(es, "junk", [128, 1024], BF16)
                W["st"] = [sb(es, "st%d" % i, [128, 8], F32) for i in range(2)]
                W["tA"] = sb(es, "tA", [128, 512], F32)
                W["tB"] = sb(es, "tB", [128, 512], F32)
                W["kro"] = sb(es, "kro", [128, 512], F32)
                W["v_bf"] = sb(es, "v_bf", [128, 1024], BF16)
                W["t0"] = pp(es, "t0", [128, 1024], BF16)
                W["t1"] = pp(es, "t1", [128, 1024], BF16)
                W["g"] = pp(es, "g", [128, 3072], F32)
                W["gb"] = [Buf("g%d" % i) for i in range(6)]
                return W

            def prep_a(W, n, i):
                xt, xn, st = W["xt"][i], W["xn"][i], W["st"][i]
                S.op("sp", DMA(xt[:, :], xs[n * 128:(n + 1) * 128, :]), writes=[xt], stream="x%d" % i)
                S.op("act", ACTF(W["junk"][:, :], xt[:, :], AF.Square, accum=st[:, 0:1]), reads=[xt], writes=[W["junk"], st])
                S.op("dve", TS(st[:, 1:2], st[:, 0:1], 1.0 / D, EPS, op0=ALU.mult, op1=ALU.add), reads=[st], writes=[st])
                S.op("pool", TT(st[:, 2:3], st[:, 1:2], neghalf[:, 0:1], ALU.pow), reads=[st], writes=[st])
                S.op("dve", TS(xn[:, :], xt[:, :], st[:, 2:3], op0=ALU.mult), reads=[xt, st], writes=[xn])

            def prep_b(W, i):
                xn, hT, t0 = W["xn"][i], W["hT"][i], W["t0"]
                for k in range(8):
                    S.op("pe", TR(t0[:, k * 128:(k + 1) * 128], xn[:, k * 128:(k + 1) * 128], identb[:, :]), reads=[xn], writes=[t0])
                for k in range(8):
                    if k % 2 == 0:
                        S.op("dve", TS(hT[:, k * 128:(k + 1) * 128], t0[:, k * 128:(k + 1) * 128], AT[:, k:k + 1], shT[:, k:k + 1],
                                       op0=ALU.mult, op1=ALU.add), reads=[t0], writes=[hT])
                    else:
                        S.op("act", ACTF(hT[:, k * 128:(k + 1) * 128], t0[:, k * 128:(k + 1) * 128], AF.Identity,
                                         scale=AT[:, k:k + 1], bias=shT[:, k:k + 1]), reads=[t0], writes=[hT])

            def proj(W, hT, c0, c1, gi, goff=0, wb=()):
                g = W["g"]
                n_ = c1 - c0
                for k in range(8):
                    S.op("pe", MM(g[:, gi * 512 + goff:gi * 512 + goff + n_], hT[:, k * 128:(k + 1) * 128], w_in_v[:, k, c0:c1],
                                  start=(k == 0), stop=(k == 7)), reads=[hT] + list(wb), writes=[W["gb"][gi]])

            def rope64(W, src_ap, src_bufs, n, dst_ap, dst_bufs, eng="dve"):
                tA, tB = W["tA"], W["tB"]
                s4 = src_ap.rearrange("p (h two f) -> p h two f", h=8, two=2)
                c_b = bc(cos3[:, n, :].unsqueeze(1).unsqueeze(1), [128, 8, 2, 32])
                s_b = bc(sin3[:, n, :].unsqueeze(1), [128, 8, 32])
                tA4 = tA[:, :].rearrange("p (h two f) -> p h two f", h=8, two=2)
                tB4 = tB[:, :].rearrange("p (h two f) -> p h two f", h=8, two=2)
                d4 = dst_ap.rearrange("p (h two f) -> p h two f", h=8, two=2)
                S.op(eng, TT(tA4, s4, c_b, ALU.mult), reads=src_bufs, writes=[tA])
                S.op(eng, TT(tB4[:, :, 0, :], s4[:, :, 1, :], s_b, ALU.mult), reads=src_bufs, writes=[tB])
                S.op(eng, TT(tB4[:, :, 1, :], s4[:, :, 0, :], s_b, ALU.mult), reads=src_bufs, writes=[tB])
                S.op(eng, TT(d4[:, :, 0, :], tA4[:, :, 0, :], tB4[:, :, 0, :], ALU.subtract), reads=[tA, tB], writes=dst_bufs)
                S.op(eng, TT(d4[:, :, 1, :], tA4[:, :, 1, :], tB4[:, :, 1, :], ALU.add), reads=[tA, tB], writes=dst_bufs)

            with ExitStack() as es:
                W = make_common(es)
                g, gb, t0, t1 = W["g"], W["gb"], W["t0"], W["t1"]
                Sdn = sb(es, "Sdn", [128, 512], F32)
                Sst = [sb(es, "Sst%d" % i, [128, 512], BF16) for i in range(2)]
                kd_bf = sb(es, "kd_bf", [128, 512], BF16)
                ckvn_bf = sb(es, "ckvn_bf", [128, 256], BF16)
                ckvnT = sb(es, "ckvnT", [128, 256], BF16)
                KR = sb(es, "KR", [128, 768], BF16)
                sqk = sb(es, "sqk", [128, 512], F32)
                kntmp = sb(es, "kntmp", [128, 512], F32)
                krtmp = sb(es, "krtmp", [128, 32], F32)
                krt2 = sb(es, "krt2", [128, 32], F32)
                krt3 = sb(es, "krt3", [128, 32], F32)
                sm = sb(es, "sm", [128, 32], F32)
                KTst = sb(es, "KTst", [128, 8 * 512], BF16)
                Vst = sb(es, "Vst", [128, 8 * 512], BF16)
                S.op("dve", MSET(Sdn[:, :], 0.0), writes=[Sdn])

                order = list(range(NB - 1, -1, -1))
                prep_a(W, order[0], 0)
                prep_b(W, 0)
                for idx, n in enumerate(order):
                    i = idx % 2
                    hT = W["hT"][i]
                    if idx + 1 < NB:
                        prep_a(W, order[idx + 1], 1 - i)
                    proj(W, hT, C_RK, C_RV, 0, wb=[win_bufs["rkv"]])
                    proj(W, hT, C_RV, C_RV + 512, 1, wb=[win_bufs["rkv"]])
                    proj(W, hT, C_RV + 512, C_RG, 2, wb=[win_bufs["rkv"]])
                    proj(W, hT, C_CKV, C_MG, 3, wb=[win_bufs["ckvkr"]])
                    if idx + 1 < NB:
                        prep_b(W, 1 - i)
                    rope64(W, g[:, 0:512], [gb[0]], n, W["kro"][:, :], [W["kro"]])
                    S.op("dve", TT(kd_bf[:, :].rearrange("p (h f) -> p h f", h=8), W["kro"][:, :].rearrange("p (h f) -> p h f", h=8),
                                   bc(kdec[:, 8:16].unsqueeze(2), [128, 8, 64]), ALU.mult), reads=[W["kro"]], writes=[kd_bf])
                    S.op("act", ACP(W["v_bf"][:, :], g[:, 512:1536]), reads=[gb[1], gb[2]], writes=[W["v_bf"]])
                    for h in range(8):
                        pr, e = h // 2, h % 2
                        S.op("pe", MM(g[e * 64:(e + 1) * 64, pr * 128:(pr + 1) * 128], kd_bf[:, h * 64:(h + 1) * 64],
                                      W["v_bf"][:, h * 128:(h + 1) * 128]), reads=[kd_bf, W["v_bf"]], writes=[gb[0]])
                    if n < NO:
                        sst = Sst[n % 2]
                        S.op("act", ACP(sst[:, :], Sdn[:, :]), reads=[Sdn], writes=[sst])
                        S.op("sp", DMA(sst_scr[n], sst[:, :]), reads=[sst], writes=[sst_b[n]], stream="sst%d" % (n % 2))
                    for pr in range(4):
                        S.op("dve", STT(Sdn[:, pr * 128:(pr + 1) * 128], Sdn[:, pr * 128:(pr + 1) * 128], dec128[:, 4 + pr:5 + pr],
                                        g[:, pr * 128:(pr + 1) * 128], ALU.mult, ALU.add), reads=[Sdn, gb[0]], writes=[Sdn])
                    st = W["st"][i]
                    S.op("act", ACTF(W["junk"][:, 0:256], g[:, 1536:1792], AF.Square, accum=st[:, 3:4]), reads=[gb[3]], writes=[W["junk"], st])
                    S.op("act", ACTF(W["junk"][:, 256:288], g[:, 1792:1824], AF.Square, accum=st[:, 4:5]), reads=[gb[3]], writes=[W["junk"], st])
                    S.op("dve", TS(st[:, 5:6], st[:, 3:4], 1.0 / 256, EPS, op0=ALU.mult, op1=ALU.add), reads=[st], writes=[st])
                    S.op("dve", TS(st[:, 6:7], st[:, 4:5], 1.0 / 32, EPS, op0=ALU.mult, op1=ALU.add), reads=[st], writes=[st])
                    S.op("pool", TT(st[:, 5:7], st[:, 5:7], neghalf[:, 0:2], ALU.pow), reads=[st], writes=[st])
                    S.op("dve", STT(ckvn_bf[:, :], g[:, 1536:1792], st[:, 5:6], kvnw_b[:, :], ALU.mult, ALU.mult), reads=[gb[3], st], writes=[ckvn_bf])
                    S.op("dve", STT(krtmp[:, :], g[:, 1792:1824], st[:, 6:7], kw_b[:, 64:96], ALU.mult, ALU.mult), reads=[gb[3], st], writes=[krtmp])
                    for c in range(2):
                        S.op("pe", TR(t1[:, c * 128:(c + 1) * 128], ckvn_bf[:, c * 128:(c + 1) * 128], identb[:, :]), reads=[ckvn_bf], writes=[t1])
                    S.op("act", ACP(ckvnT[:, :], t1[:, 0:256]), reads=[t1], writes=[ckvnT])
                    for nt in range(3):
                        for c in range(2):
                            S.op("pe", MM(g[:, (3 + nt) * 512:(4 + nt) * 512], ckvnT[:, c * 128:(c + 1) * 128],
                                          w_ukv_bf[:, c * 1536 + nt * 512:c * 1536 + (nt + 1) * 512], start=(c == 0), stop=(c == 1)),
                                 reads=[ckvnT, wukv_b], writes=[gb[3 + nt]])
                    cs16 = cos3[:, n, :].rearrange("p (f two) -> p f two", two=2)[:, :, 0]
                    sn16 = sin3[:, n, :].rearrange("p (f two) -> p f two", two=2)[:, :, 0]
                    kr3 = krtmp[:, :].rearrange("p (two f) -> p two f", two=2)
                    S.op("pool", TT(krt2[:, :].rearrange("p (two f) -> p two f", two=2), kr3, bc(cs16.unsqueeze(1), [128, 2, 16]), ALU.mult),
                         reads=[krtmp], writes=[krt2])
                    S.op("pool", TT(krt3[:, 0:16], krtmp[:, 16:32], sn16, ALU.mult), reads=[krtmp], writes=[krt3])
                    S.op("pool", TT(krt3[:, 16:32], krtmp[:, 0:16], sn16, ALU.mult), reads=[krtmp], writes=[krt3])
                    S.op("pool", TT(krt2[:, 0:16], krt2[:, 0:16], krt3[:, 0:16], ALU.subtract), reads=[krt2, krt3], writes=[krt2])
                    S.op("pool", TT(krt2[:, 16:32], krt2[:, 16:32], krt3[:, 16:32], ALU.add), reads=[krt2, krt3], writes=[krt2])
                    KR3 = KR[:, :].rearrange("p (h f) -> p h f", h=8)
                    S.op("pool", CP(KR3[:, :, 64:96], bc(krt2[:, :].unsqueeze(1), [128, 8, 32])), reads=[krt2], writes=[KR])
                    kv3 = g[:, 1536:3072].rearrange("p (h f) -> p h f", h=8)
                    kvb = [gb[3], gb[4], gb[5]]
                    S.op("act", ACTF(sqk[:, :].rearrange("p (h f) -> p h f", h=8), kv3[:, :, 0:64], AF.Square), reads=kvb, writes=[sqk])
                    S.op("dve", RED(sm[:, 0:8], sqk[:, :].rearrange("p (h f) -> p h f", h=8)), reads=[sqk], writes=[sm])
                    S.op("dve", TS(sm[:, 8:16], sm[:, 0:8], 1.0 / 64, EPS, op0=ALU.mult, op1=ALU.add), reads=[sm], writes=[sm])
                    S.op("pool", TT(sm[:, 16:24], sm[:, 8:16], neghalf[:, 0:8], ALU.pow), reads=[sm], writes=[sm])
                    S.op("dve", TT(kntmp[:, :].rearrange("p (h f) -> p h f", h=8), kv3[:, :, 0:64], bc(sm[:, 16:24].unsqueeze(2), [128, 8, 64]), ALU.mult),
                         reads=kvb + [sm], writes=[kntmp])
                    S.op("pool", TT(KR3[:, :, 0:64], kntmp[:, :].rearrange("p (h f) -> p h f", h=8), bc(kw_b[:, 0:64].unsqueeze(1), [128, 8, 64]), ALU.mult),
                         reads=[kntmp], writes=[KR])
                    j = n % 4
                    gidx = n // 4
                    Vst4 = Vst[:, :].rearrange("p (h j f) -> p h j f", h=8, j=4)
                    S.op("act", ACP(Vst4[:, :, j, :], kv3[:, :, 64:192]), reads=kvb, writes=[Vst])
                    for h in range(8):
                        S.op("pe", TR(t1[0:96, h * 128:(h + 1) * 128], KR[:, h * 96:(h + 1) * 96], identb[:, :]), reads=[KR], writes=[t1])
                    KTst3 = KTst[:, :].rearrange("p (h t) -> p h t", h=8)
                    S.op("dve", CP(KTst3[0:96, :, j * 128:(j + 1) * 128], t1[0:96, :].rearrange("p (h t) -> p h t", h=8)), reads=[t1], writes=[KTst])
                    if j == 0:
                        S.op("sp", DMA(kt_scr[:, :, gidx * 512:(gidx + 1) * 512].rearrange("h d t -> d h t"), KTst3[0:96, :, :]),
                             reads=[KTst], writes=[kt_b[gidx]], stream="kt")
                        S.op("sp", DMA(v_scr[:, :, gidx * 4:(gidx + 1) * 4, :].rearrange("h p j f -> p h j f"), Vst4),
                             reads=[Vst], writes=[v_b[gidx]], stream="vv")
                    if debug and n == 0:
                        S.op("sp", DMA(dbg["hT0"], hT[:, :]), reads=[hT], stream="dbg")
                S.emit_block()
            if stop == "R":
                return nc

            with ExitStack() as es:
                W = make_common(es)
                g, gb, t0, t1 = W["g"], W["gb"], W["t0"], W["t1"]
                Sup = sb(es, "Sup", [128, 512], F32)
                Sup_bf = sb(es, "Sup_bf", [128, 512], BF16)
                Sdl = [sb(es, "Sdl%d" % i, [128, 512], BF16) for i in range(2)]
                qro = sb(es, "qro", [128, 512], BF16)
                kr_bf = sb(es, "kr_bf", [128, 512], BF16)
                ku_bf = sb(es, "ku_bf", [128, 512], BF16)
                sg = sb(es, "sg", [128, 1024], BF16)
                qT = sb(es, "qT", [128, 512], BF16)
                qTm = [sb(es, "qTm%d" % e_, [128, 512], BF16) for e_ in range(2)]
                qupT = [sb(es, "qupT%d" % e_, [128, 512], BF16) for e_ in range(2)]
                qdnT = [sb(es, "qdnT%d" % e_, [128, 512], BF16) for e_ in range(2)]
                kT = sb(es, "kT", [128, 512], BF16)
                AD = sb(es, "AD", [128, 1024], BF16)
                sq = sb(es, "sq", [128, 1024], F32)
                g1 = sb(es, "g1", [128, 1024], F32)
                ro_bf = sb(es, "ro_bf", [128, 1024], BF16)
                roT = sb(es, "roT", [128, 1024], BF16)
                mgst = sb(es, "mgst", [128, 1024], BF16)
                cqn_bf = sb(es, "cqn_bf", [128, 384], BF16)
                cqnT = sb(es, "cqnT", [128, 384], BF16)
                QR = sb(es, "QR", [128, 768], BF16)
                sqq = sb(es, "sqq", [128, 768], F32)
                qtmp = sb(es, "qtmp", [128, 768], F32)
                qrt2 = sb(es, "qrt2", [128, 256], F32)
                qrt3 = sb(es, "qrt3", [128, 256], F32)
                gs = sb(es, "gs", [128, 64], F32)
                sm = sb(es, "sm2", [128, 64], F32)
                QTst = sb(es, "QTst", [128, 8 * 512], BF16)
                if c2mode != "none":
                    S.op("dve", MSET(Sup[:, :], 0.0), writes=[Sup])
                    S.op("dve", MSET(Sup_bf[:, :], 0.0), writes=[Sup_bf])
                if c2mode == "none":
                    S.stream_last = {}
                rgcq = [win_bufs["rgcq"]]

                if c2mode == "nodrain":
                    S.stream_last = {}
                if c2mode not in ("empty", "sdl", "none", "nodrain"):
                    prep_a(W, 0, 0)
                    prep_b(W, 0)
                for n in range(NO):
                    if c2mode in ("empty", "none", "nodrain"):
                        break
                    i = n % 2
                    hT = W["hT"][i]
                    st = W["st"][i]
                    sdl = Sdl[i]
                    if c2mode != "prep":
                        S.op("sp", DMA(sdl[:, :], sst_scr[n]), reads=[sst_b[n]], writes=[sdl], stream="sdl%d" % i)
                    if c2mode == "sdl":
                        continue
                    if n + 1 < NO:
                        prep_a(W, n + 1, 1 - i)
                    if c2cut <= -2:
                        if n + 1 < NO:
                            prep_b(W, 1 - i)
                        continue
                    proj(W, hT, C_RQ, C_RK, 0, wb=[win_bufs["rq"]])
                    proj(W, hT, C_RK, C_RV, 1, wb=[win_bufs["rkv"]])
                    proj(W, hT, C_RV, C_RV + 512, 2)
                    proj(W, hT, C_RV + 512, C_RG, 3)
                    proj(W, hT, C_RG, C_RG + 512, 4, wb=rgcq)
                    proj(W, hT, C_RG + 512, C_CQ, 5, wb=rgcq)
                    if c2cut <= -1:
                        if n + 1 < NO:
                            prep_b(W, 1 - i)
                        continue
                    rope64(W, g[:, 0:512], [gb[0]], n, qro[:, :], [qro])
                    rope64(W, g[:, 512:1024], [gb[1]], n, W["kro"][:, :], [W["kro"]])
                    S.op("pool", CP(kr_bf[:, :], W["kro"][:, :]), reads=[W["kro"]], writes=[kr_bf])
                    S.op("pool", TT(ku_bf[:, :].rearrange("p (h f) -> p h f", h=8), W["kro"][:, :].rearrange("p (h f) -> p h f", h=8),
                                    bc(kdec[:, 0:8].unsqueeze(2), [128, 8, 64]), ALU.mult), reads=[W["kro"]], writes=[ku_bf])
                    S.op("act", ACP(W["v_bf"][:, :], g[:, 1024:2048]), reads=[gb[2], gb[3]], writes=[W["v_bf"]])
                    S.op("act", ACTF(sg[:, :], g[:, 2048:3072], AF.Silu), reads=[gb[4], gb[5]], writes=[sg])
                    if c2cut <= 0:
                        if n + 1 < NO:
                            prep_b(W, 1 - i)
                        continue
                    for pr in range(4):
                        S.op("pe", TR(t1[:, pr * 128:(pr + 1) * 128], qro[:, pr * 128:(pr + 1) * 128], identb[:, :]), reads=[qro], writes=[t1])
                    for pr in range(4):
                        S.op("pe", TR(t1[:, 512 + pr * 128:512 + (pr + 1) * 128], kr_bf[:, pr * 128:(pr + 1) * 128], identb[:, :]), reads=[kr_bf], writes=[t1])
                    proj(W, hT, C_CQ, C_CKV, 0, wb=rgcq)
                    S.op("act", ACP(qT[:, :], t1[:, 0:512]), reads=[t1], writes=[qT])
                    for e_ in range(2):
                        S.op("dve", TS(qTm[e_][:, :], qT[:, :], constT[:, 674 + e_:675 + e_], op0=ALU.mult), reads=[qT], writes=[qTm[e_]])
                        S.op("pool", TT(qupT[e_][:, :], qTm[e_][:, :], DqT[:, 0:512], ALU.mult), reads=[qTm[e_]], writes=[qupT[e_]])
                        S.op("dve", TT(qdnT[e_][:, :], qTm[e_][:, :], DqT[:, 512:1024], ALU.mult), reads=[qTm[e_]], writes=[qdnT[e_]])
                    S.op("act", ACP(kT[:, :], t1[:, 512:1024]), reads=[t1], writes=[kT])
                    if c2cut <= 1:
                        if n + 1 < NO:
                            prep_b(W, 1 - i)
                        continue
                    for h in (0, 2, 4, 6, 1, 3, 5, 7):
                        pr, e = h // 2, h % 2
                        S.op("pe", MM(g[:, 1024 + h * 128:1024 + (h + 1) * 128], kT[:, pr * 128:(pr + 1) * 128],
                                      qTm[e][:, pr * 128:(pr + 1) * 128]), reads=[kT, qTm[e]], writes=[gb[2 + h // 4]])
                    S.op("dve", TT(AD[:, 0:512], g[:, 1024:1536], DT[:, 0:512], ALU.mult), reads=[gb[2]], writes=[AD])
                    S.op("dve", TT(AD[:, 512:1024], g[:, 1536:2048], DT[:, 512:1024], ALU.mult), reads=[gb[3]], writes=[AD])
                    for h in range(8):
                        pr, e = h // 2, h % 2
                        ob = gb[4 + h // 4]
                        oap = g[:, 2048 + h * 128:2048 + (h + 1) * 128]
                        S.op("pe", MM(oap, AD[:, h * 128:(h + 1) * 128], W["v_bf"][:, h * 128:(h + 1) * 128], start=True, stop=False),
                             reads=[AD, W["v_bf"]], writes=[ob])
                        S.op("pe", MM(oap, qupT[e][:, pr * 128:(pr + 1) * 128], Sup_bf[:, pr * 128:(pr + 1) * 128],
                                      start=False, stop=False), reads=[qupT[e], Sup_bf], writes=[ob])
                        S.op("pe", MM(oap, qdnT[e][:, pr * 128:(pr + 1) * 128], sdl[:, pr * 128:(pr + 1) * 128],
                                      start=False, stop=True), reads=[qdnT[e], sdl], writes=[ob])
                    for h in range(8):
                        pr, e = h // 2, h % 2
                        S.op("pe", MM(g[e * 64:(e + 1) * 64, 512 + pr * 128:512 + (pr + 1) * 128], ku_bf[:, h * 64:(h + 1) * 64],
                                      W["v_bf"][:, h * 128:(h + 1) * 128]), reads=[ku_bf, W["v_bf"]], writes=[gb[1]])
                    for pr in range(4):
                        S.op("dve", STT(Sup[:, pr * 128:(pr + 1) * 128], Sup[:, pr * 128:(pr + 1) * 128], dec128[:, pr:pr + 1],
                                        g[:, 512 + pr * 128:512 + (pr + 1) * 128], ALU.mult, ALU.add), reads=[Sup, gb[1]], writes=[Sup])
                    S.op("act", ACP(Sup_bf[:, :], Sup[:, :]), reads=[Sup], writes=[Sup_bf])
                    if c2cut <= 2:
                        if n + 1 < NO:
                            prep_b(W, 1 - i)
                        continue
                    S.op("act", ACTF(W["junk"][:, 0:384], g[:, 0:384], AF.Square, accum=st[:, 3:4]), reads=[gb[0]], writes=[W["junk"], st])
                    S.op("dve", TS(st[:, 5:6], st[:, 3:4], 1.0 / 384, EPS, op0=ALU.mult, op1=ALU.add), reads=[st], writes=[st])
                    S.op("pool", TT(st[:, 5:6], st[:, 5:6], neghalf[:, 0:1], ALU.pow), reads=[st], writes=[st])
                    S.op("dve", STT(cqn_bf[:, :], g[:, 0:384], st[:, 5:6], qnw_b[:, :], ALU.mult, ALU.mult), reads=[gb[0], st], writes=[cqn_bf])
                    for c in range(3):
                        S.op("pe", TR(t0[:, c * 128:(c + 1) * 128], cqn_bf[:, c * 128:(c + 1) * 128], identb[:, :]), reads=[cqn_bf], writes=[t0])
                    S.op("act", ACP(cqnT[:, :], t0[:, 0:384]), reads=[t0], writes=[cqnT])
                    if n + 1 < NO:
                        pass
                    for (c0, c1, gi) in ((0, 512, 2), (512, 768, 3)):
                        for c in range(3):
                            S.op("pe", MM(g[:, gi * 512:gi * 512 + (c1 - c0)], cqnT[:, c * 128:(c + 1) * 128],
                                          w_uq_bf[:, c * 768 + c0:c * 768 + c1], start=(c == 0), stop=(c == 2)),
                                 reads=[cqnT, wuq_b], writes=[gb[gi]])
                    O3 = g[:, 2048:3072].rearrange("p (h v) -> p h v", h=8)
                    ob2 = [gb[4], gb[5]]
                    S.op("dve", RED(gs[:, 0:8], O3), reads=ob2, writes=[gs])
                    S.op("act", ACTF(sq[:, :], g[:, 2048:3072], AF.Square), reads=ob2, writes=[sq])
                    S.op("dve", RED(gs[:, 8:16], sq[:, :].rearrange("p (h v) -> p h v", h=8)), reads=[sq], writes=[gs])
                    S.op("dve", TS(gs[:, 16:24], gs[:, 0:8], 1.0 / 128, op0=ALU.mult), reads=[gs], writes=[gs])
                    S.op("dve", TT(gs[:, 24:32], gs[:, 16:24], gs[:, 16:24], ALU.mult), reads=[gs], writes=[gs])
                    S.op("dve", STT(gs[:, 32:40], gs[:, 8:16], 1.0 / 128, gs[:, 24:32], ALU.mult, ALU.subtract), reads=[gs], writes=[gs])
                    S.op("dve", TS(gs[:, 32:40], gs[:, 32:40], EPS, op0=ALU.add), reads=[gs], writes=[gs])
                    S.op("pool", TT(gs[:, 40:48], gs[:, 32:40], neghalf[:, 0:8], ALU.pow), reads=[gs], writes=[gs])
                    S.op("dve", STT(gs[:, 48:56], gs[:, 16:24], -1.0, gs[:, 40:48], ALU.mult, ALU.mult), reads=[gs], writes=[gs])
                    S.op("pool", TT(g1[:, :], gnw_b[:, :], sg[:, :], ALU.mult), reads=[sg], writes=[g1])
                    for h in range(8):
                        S.op("act", ACTF(sq[:, h * 128:(h + 1) * 128], g[:, 2048 + h * 128:2048 + (h + 1) * 128], AF.Identity,
                                         scale=gs[:, 40 + h:41 + h], bias=gs[:, 48 + h:49 + h]), reads=ob2 + [gs], writes=[sq])
                    S.op("pool", TT(ro_bf[:, :], sq[:, :], g1[:, :], ALU.mult), reads=[sq, g1], writes=[ro_bf])
                    for k in range(8):
                        S.op("pe", TR(t0[:, k * 128:(k + 1) * 128], ro_bf[:, k * 128:(k + 1) * 128], identb[:, :]), reads=[ro_bf], writes=[t0])
                    S.op("dve", CP(roT[:, :], t0[:, :]), reads=[t0], writes=[roT])
                    S.op("sp", DMA(ret_scr[n], roT[:, :]), reads=[roT], writes=[ret_b[n]], stream="ret")
                    if c2cut <= 3:
                        if n + 1 < NO:
                            prep_b(W, 1 - i)
                        continue
                    q3 = g[:, 1024:1792].rearrange("p (h f) -> p h f", h=8)
                    qb = [gb[2], gb[3]]
                    sqq3 = sqq[:, :].rearrange("p (h f) -> p h f", h=8)
                    qtmp3 = qtmp[:, :].rearrange("p (h f) -> p h f", h=8)
                    QR3 = QR[:, :].rearrange("p (h f) -> p h f", h=8)
                    S.op("act", ACTF(sqq[:, :], g[:, 1024:1792], AF.Square), reads=qb, writes=[sqq])
                    S.op("dve", RED(sm[:, 0:8], sqq3[:, :, 0:64]), reads=[sqq], writes=[sm])
                    S.op("dve", RED(sm[:, 8:16], sqq3[:, :, 64:96]), reads=[sqq], writes=[sm])
                    S.op("dve", TS(sm[:, 0:8], sm[:, 0:8], 1.0 / 64, EPS, op0=ALU.mult, op1=ALU.add), reads=[sm], writes=[sm])
                    S.op("dve", TS(sm[:, 8:16], sm[:, 8:16], 1.0 / 32, EPS, op0=ALU.mult, op1=ALU.add), reads=[sm], writes=[sm])
                    S.op("pool", TT(sm[:, 16:32], sm[:, 0:16], neghalf[:, 0:16], ALU.pow), reads=[sm], writes=[sm])
                    S.op("dve", TT(qtmp3[:, :, 0:64], q3[:, :, 0:64], bc(sm[:, 16:24].unsqueeze(2), [128, 8, 64]), ALU.mult), reads=qb + [sm], writes=[qtmp])
                    S.op("dve", TT(qtmp3[:, :, 64:96], q3[:, :, 64:96], bc(sm[:, 24:32].unsqueeze(2), [128, 8, 32]), ALU.mult), reads=qb + [sm], writes=[qtmp])
                    S.op("pool", TT(QR3[:, :, 0:64], qtmp3[:, :, 0:64], bc(qw_b[:, 0:64].unsqueeze(1), [128, 8, 64]), ALU.mult), reads=[qtmp], writes=[QR])
                    S.op("pool", TT(qtmp3[:, :, 64:96], qtmp3[:, :, 64:96], bc(qw_b[:, 64:96].unsqueeze(1), [128, 8, 32]), ALU.mult), reads=[qtmp], writes=[qtmp])
                    cs16 = cos3[:, n, :].rearrange("p (f two) -> p f two", two=2)[:, :, 0]
                    sn16 = sin3[:, n, :].rearrange("p (f two) -> p f two", two=2)[:, :, 0]
                    qr4 = qtmp3[:, :, 64:96].rearrange("p h (two f) -> p h two f", two=2)
                    qrt24 = qrt2[:, :].rearrange("p (h two f) -> p h two f", h=8, two=2)
                    qrt34 = qrt3[:, :].rearrange("p (h two f) -> p h two f", h=8, two=2)
                    QRr4 = QR3[:, :, 64:96].rearrange("p h (two f) -> p h two f", two=2)
                    S.op("dve", TT(qrt24, qr4, bc(cs16.unsqueeze(1).unsqueeze(1), [128, 8, 2, 16]), ALU.mult), reads=[qtmp], writes=[qrt2])
                    S.op("dve", TT(qrt34[:, :, 0, :], qr4[:, :, 1, :], bc(sn16.unsqueeze(1), [128, 8, 16]), ALU.mult), reads=[qtmp], writes=[qrt3])
                    S.op("dve", TT(qrt34[:, :, 1, :], qr4[:, :, 0, :], bc(sn16.unsqueeze(1), [128, 8, 16]), ALU.mult), reads=[qtmp], writes=[qrt3])
                    S.op("dve", TT(QRr4[:, :, 0, :], qrt24[:, :, 0, :], qrt34[:, :, 0, :], ALU.subtract), reads=[qrt2, qrt3], writes=[QR])
                    S.op("dve", TT(QRr4[:, :, 1, :], qrt24[:, :, 1, :], qrt34[:, :, 1, :], ALU.add), reads=[qrt2, qrt3], writes=[QR])
                    for h in range(8):
                        S.op("pe", TR(t1[0:96, h * 128:(h + 1) * 128], QR[:, h * 96:(h + 1) * 96], identb[:, :]), reads=[QR], writes=[t1])
                    j = n % 4
                    gidx = n // 4
                    QTst3 = QTst[:, :].rearrange("p (h t) -> p h t", h=8)
                    S.op("dve", CP(QTst3[0:96, :, j * 128:(j + 1) * 128], t1[0:96, :].rearrange("p (h t) -> p h t", h=8)), reads=[t1], writes=[QTst])
                    if j == 3:
                        S.op("sp", DMA(qt_scr[:, :, gidx * 512:(gidx + 1) * 512].rearrange("h d t -> d h t"), QTst3[0:96, :, :]),
                             reads=[QTst], writes=[qt_b[gidx]], stream="qt")
                    if c2cut <= 4:
                        if n + 1 < NO:
                            prep_b(W, 1 - i)
                        continue
                    for c in range(8):
                        for k in range(8):
                            S.op("pe", MM(g[:, 2048 + c * 128:2048 + (c + 1) * 128], w_in_v[:, k, C_MG + c * 128:C_MG + (c + 1) * 128],
                                          hT[:, k * 128:(k + 1) * 128], start=(k == 0), stop=(k == 7)),
                                 reads=[hT, win_bufs["mg"]], writes=[gb[4 + c // 4]])
                    S.op("act", ACTF(mgst[:, :], g[:, 2048:3072], AF.Silu), reads=ob2, writes=[mgst])
                    S.op("sp", DMA(mg_scr[n], mgst[:, :]), reads=[mgst], writes=[mg_b[n]], stream="mg")
                    if n + 1 < NO:
                        prep_b(W, 1 - i)
                S.emit_block()
            if stop == "C2":
                return nc

        with ExitStack() as es2:
            attT = sb(es2, "attT", [128, 8 * TO], BF16)
            w_out_bf = sb(es2, "w_out_bf", [128, 16 * 1024], BF16)
            wout_b = Buf("wout")
            S.op("pool", DMA(w_out_bf[:, :].rearrange("p (k n) -> p k n", k=16), w_out.rearrange("(k p) n -> p k n", p=128)),
                 writes=[wout_b], stream="w_out")
            with ExitStack() as es:
                KTh = [sb(es, "KTh%d" % i, [128, S_LEN], BF16) for i in range(2)]
                Vh = [sb(es, "Vh%d" % i, [128, NB * 128], BF16) for i in range(2)]
                QTh = [sb(es, "QTh%d" % i, [128, TO], BF16) for i in range(2)]
                NPT = 6
                PT = [sb(es, "PT%d" % i, [128, QT], BF16) for i in range(NPT)]
                s4 = [sb(es, "s4_%d" % i, [128, QT], BF16) for i in range(2)]
                rden = sb(es, "rden", [128, QT], F32)
                pa = pp(es, "pa", [128, 4096], F32)
                sab = [Buf("sa0"), Buf("sa1")]
                oab = Buf("oa")
                dnb = Buf("dn")
                pti = 0
                for h in range(8):
                    i = h % 2
                    kth, vh, qth = KTh[i], Vh[i], QTh[i]
                    S.op("sp", DMA(kth[0:96, :], kt_scr[h]), reads=kt_b, writes=[kth], stream="kth%d" % i)
                    S.op("sp", DMA(vh[:, :].rearrange("p (j f) -> p j f", f=128), v_scr[h]), reads=v_b, writes=[vh], stream="vh%d" % i)
                    S.op("sp", DMA(qth[0:96, :], qt_scr[h]), reads=qt_b, writes=[qth], stream="qth%d" % i)
                    for qt in range(NQT):
                        q0 = qt * QT
                        for j in range(NB):
                            sa = pa[:, (j % 2) * 1024:(j % 2 + 1) * 1024]
                            sbuf_ = sab[j % 2]
                            for hf in range(2):
                                S.op("pe", MM(sa[:, hf * 512:(hf + 1) * 512], kth[0:96, j * 128:(j + 1) * 128],
                                              qth[0:96, q0 + hf * 512:q0 + (hf + 1) * 512]), reads=[kth, qth], writes=[sbuf_])
                            pt = PT[pti % NPT]
                            pti += 1
                            S.op("act", ACTF(pt[:, :], sa, AF.Exp), reads=[sbuf_], writes=[pt])
                            for hf in range(2):
                                S.op("pe", MM(pa[:, 2048 + hf * 512:2048 + (hf + 1) * 512], vh[:, j * 128:(j + 1) * 128],
                                              pt[:, hf * 512:(hf + 1) * 512], start=(j == 0), stop=(j == NB - 1)), reads=[vh, pt], writes=[oab])
                            qd = j % 4
                            sidx = (j // 4) % 2
                            if qd == 1:
                                S.op("dve", TT(s4[sidx][:, :], PT[(pti - 2) % NPT][:, :], pt[:, :], ALU.add),
                                     reads=[PT[(pti - 2) % NPT], pt], writes=[s4[sidx]])
                            elif qd == 2:
                                S.op("dve", TT(s4[sidx][:, :], s4[sidx][:, :], pt[:, :], ALU.add),
                                     reads=[s4[sidx], pt], writes=[s4[sidx]])
                            elif qd == 3:
                                S.op("dve", TT(s4[sidx][:, :], s4[sidx][:, :], pt[:, :], ALU.add),
                                     reads=[s4[sidx], pt], writes=[s4[sidx]])
                                for hf in range(2):
                                    S.op("pe", MM(pa[:, 3072 + hf * 512:3072 + (hf + 1) * 512], onesb[:, :], s4[sidx][:, hf * 512:(hf + 1) * 512],
                                                  start=(j == 3), stop=(j == NB - 1)), reads=[s4[sidx]], writes=[dnb])
                        S.op("dve", lambda e: e.reciprocal(out=rden[:, :], in_=pa[:, 3072:4096]), reads=[dnb], writes=[rden])
                        S.op("dve", TT(attT[:, h * TO + q0:h * TO + q0 + QT], pa[:, 2048:3072], rden[:, :], ALU.mult),
                             reads=[oab, rden], writes=[attT])
                if debug:
                    S.op("sp", DMA(dbg["attT"], attT[:, :]), reads=[attT], stream="dbg")
                S.emit_block()
            if stop == "attn":
                return nc

            with ExitStack() as es:
                xt = [sb(es, "fx%d" % i, [128, 1024], F32) for i in range(2)]
                rT = [sb(es, "frT%d" % i, [128, 1024], BF16) for i in range(2)]
                mgT = [sb(es, "fmg%d" % i, [128, 1024], BF16) for i in range(2)]
                sqa = sb(es, "fsq", [128, 1024], BF16)
                rs = sb(es, "frs", [128, 128], F32)
                m1 = sb(es, "fm1", [128, 1024], F32)
                m2 = sb(es, "fm2", [128, 1024], F32)
                mT = sb(es, "fmT", [128, 1024], BF16)
                yo = [sb(es, "fyo%d" % i, [128, 1024], F32) for i in range(2)]
                py = pp(es, "py", [128, 2048], F32)
                pn = pp(es, "pn", [128, 512], F32)
                pyb = [Buf("py0"), Buf("py1")]
                attT3 = attT[:, :].rearrange("p (h t) -> p h t", h=8)
                for n in range(NO):
                    i = n % 2
                    S.op("sp", DMA(xt[i][:, :], xs[n * 128:(n + 1) * 128, :]), writes=[xt[i]], stream="x%d" % i)
                    S.op("sp", DMA(rT[i][:, :], ret_scr[n]), reads=[ret_b[n]], writes=[rT[i]], stream="frT%d" % i)
                    S.op("sp", DMA(mgT[i][:, :], mg_scr[n]), reads=[mg_b[n]], writes=[mgT[i]], stream="fmg%d" % i)
                    a3 = attT3[:, :, n * 128:(n + 1) * 128]
                    S.op("pool", TT(sqa[:, :].rearrange("p (h t) -> p h t", h=8), a3, a3, ALU.mult), reads=[attT], writes=[sqa])
                    for h in range(8):
                        S.op("pe", MM(pn[:, 0:128], onesb[:, :], sqa[:, h * 128:(h + 1) * 128], start=(h == 0), stop=(h == 7)), reads=[sqa], writes=[pn])
                    S.op("dve", TS(rs[:, :], pn[:, 0:128], 1.0 / 1024, EPS, op0=ALU.mult, op1=ALU.add), reads=[pn], writes=[rs])
                    S.op("pool", TT(rs[:, :], rs[:, :], bc(neghalf[:, 0:1], [128, 128]), ALU.pow), reads=[rs], writes=[rs])
                    S.op("dve", TT(m1[:, :].rearrange("p (h t) -> p h t", h=8), a3, bc(rs[:, :].unsqueeze(1), [128, 8, 128]), ALU.mult),
                         reads=[attT, rs], writes=[m1])
                    S.op("pool", TT(m2[:, :].rearrange("p (h t) -> p h t", h=8), mgT[i][:, :].rearrange("p (h t) -> p h t", h=8),
                                    bc(mlawT[:, :].unsqueeze(2), [128, 8, 128]), ALU.mult), reads=[mgT[i]], writes=[m2])
                    S.op("dve", TT(mT[:, :], m1[:, :], m2[:, :], ALU.mult), reads=[m1, m2], writes=[mT])
                    for nt in range(2):
                        yap = py[:, (i * 2 + nt) * 512:(i * 2 + nt + 1) * 512]
                        for c in range(16):
                            lhs = rT[i][:, c * 128:(c + 1) * 128] if c < 8 else mT[:, (c - 8) * 128:(c - 7) * 128]
                            S.op("pe", MM(yap, lhs, w_out_bf[:, c * 1024 + nt * 512:c * 1024 + (nt + 1) * 512], start=(c == 0), stop=(c == 15)),
                                 reads=[rT[i], mT, wout_b], writes=[pyb[i]])
                    S.op("dve", TT(yo[i][:, :], py[:, i * 1024:(i + 1) * 1024], gate_b[:, :], ALU.mult), reads=[pyb[i]], writes=[yo[i]])
                    S.op("pool", TT(yo[i][:, :], yo[i][:, :], xt[i][:, :], ALU.add), reads=[yo[i], xt[i]], writes=[yo[i]])
                    S.op("sp", DMA(out[n * 128:(n + 1) * 128, :], yo[i][:, :]), reads=[yo[i]], stream="out%d" % i)
                S.emit_block()
    return nc


def make_consts():
    c = np.zeros((128, NCONST), np.float32)
    j = np.arange(128, dtype=np.float32)[:, None]
    i = np.arange(128, dtype=np.float32)[None, :]
    c[:, 0:128] = np.eye(128, dtype=np.float32)
    c[:, 128:256] = np.maximum(i - j, 0.0)
    c[:, 256:384] = np.maximum(j - i, 0.0)
    c[:, 384:512] = np.broadcast_to(i + 1.0, (128, 128))
    c[:, 512:640] = np.broadcast_to(128.0 - i, (128, 128))
    invf = (np.float32(10000.0) ** (-np.arange(0, 64, 2, dtype=np.float32) / np.float32(64))).astype(np.float32)
    c[:, 640:672] = invf[None, :]
    c[:, 672] = 127.0 - np.arange(128, dtype=np.float32)
    c[:, 673] = np.arange(128, dtype=np.float32)
    c[:64, 674] = 1.0
    c[64:, 675] = 1.0
    return c


_NC_CACHE = {}


def prepare_inputs(inputs, S_LEN):
    f32 = np.float32
    x = np.asarray(inputs["x"], f32)
    c = np.asarray(inputs["c"], f32)
    pos = np.asarray(inputs["positions"], np.int32)
    NB = S_LEN // 128
    consts = make_consts()
    shared = {
        "consts": consts,
        "nwT": np.ascontiguousarray(np.asarray(inputs["norm_w"], f32).reshape(8, 128).T),
        "mlawT": np.ascontiguousarray(np.asarray(inputs["mla_out_norm_w"], f32).reshape(8, 128).T),
        "w_ada": np.ascontiguousarray(np.asarray(inputs["w_ada"], f32)),
        "b_ada": np.asarray(inputs["b_ada"], f32).reshape(1, -1),
        "w_in": np.ascontiguousarray(np.asarray(inputs["w_in"], f32)),
        "gnw": np.asarray(inputs["ret_gn_w"], f32).reshape(1, -1),
        "qnw": np.asarray(inputs["q_norm_w"], f32).reshape(1, -1),
        "kvnw": np.asarray(inputs["kv_norm_w"], f32).reshape(1, -1),
        "qw": np.concatenate([np.asarray(inputs["qn_nope_w"], f32), np.asarray(inputs["qn_rope_w"], f32)]).reshape(1, -1),
        "kw": np.concatenate([np.asarray(inputs["kn_nope_w"], f32), np.asarray(inputs["kn_rope_w"], f32)]).reshape(1, -1),
        "w_uq": np.ascontiguousarray(np.asarray(inputs["w_uq"], f32)),
        "w_ukv": np.ascontiguousarray(np.asarray(inputs["w_ukv"], f32)),
        "w_out": np.ascontiguousarray(np.asarray(inputs["w_out"], f32)),
    }
    lf = np.asarray(inputs["ret_decay_logit_fwd"], f32).reshape(1, 8)
    lb = np.asarray(inputs["ret_decay_logit_bwd"], f32).reshape(1, 8)
    in_maps = []
    for core in range(8):
        b, half = core // 2, core % 2
        if half == 0:
            xs_ = x[b]
            ps_ = pos[b]
            up, dn = lf, lb
        else:
            xs_ = x[b, ::-1]
            ps_ = pos[b, ::-1]
            up, dn = lb, lf
        m = dict(shared)
        m["xs"] = np.ascontiguousarray(xs_)
        m["posT"] = np.ascontiguousarray(ps_.reshape(NB, 128).T)
        m["cT"] = np.ascontiguousarray(c[b].reshape(8, 128).T)
        m["lg_up"] = up
        m["lg_dn"] = dn
        in_maps.append(m)
    return in_maps


def assemble(results, B, S_LEN):
    TO = S_LEN // 2
    out = np.empty((B, S_LEN, D), np.float32)
    for core in range(8):
        b, half = core // 2, core % 2
        o = results[core]["out"]
        if half == 0:
            out[b, :TO] = o
        else:
            out[b, TO:] = o[::-1]
    return out


def kernel(**inputs):
    x = np.asarray(inputs["x"])
    B, S_LEN, _ = x.shape
    assert B == 4
    if S_LEN not in _NC_CACHE:
        _NC_CACHE[S_LEN] = build_program(S_LEN)
    nc = _NC_CACHE[S_LEN]
    in_maps = prepare_inputs(inputs, S_LEN)
    res = run_bass_kernel_spmd(nc, in_maps, core_ids=list(range(8)))
    return assemble(res.results, B, S_LEN)
```
